# Optimizing a Trainium2 kernel written in Bass

```python
import math
import jax, jax.numpy as jnp
from jax import lax
import numpy as np

D_MODEL = 1024
BATCH = 2
SEQ = 16384
DEPTH = 4

HEAD_DIM = 64
GRID_W = 64
Q_BLOCK = 128
RMS_EPS = 1e-6
LN_EPS = 1e-5
A_HEADS = 4
A_KV_HEADS = 2
A_GROUP = A_HEADS // A_KV_HEADS
ROPE_THETA = 10000.0
B_HEADS = 4
B_QK_DIM = HEAD_DIM // 2
C_HEADS = 4
NA_ROWS = 8
NA_COLS = 16
D_GROUPS = 4
D_CHUNK = 128
D_WIDTH = D_GROUPS * HEAD_DIM
FFN_DIM = -(-8 * D_MODEL // (3 * 256)) * 256
PLE_DIM = 256

A_Q = A_HEADS * HEAD_DIM
A_KV = A_KV_HEADS * HEAD_DIM
B_QK = B_HEADS * 2 * B_QK_DIM
B_V = B_HEADS * HEAD_DIM
C_W = C_HEADS * HEAD_DIM
MIX_WIDTH = A_Q + B_V + C_W + D_WIDTH
PROJ_SPLITS = (A_Q, A_KV, A_KV, B_QK, B_QK, B_V, C_W, C_W, C_W, 2 * D_WIDTH)
PROJ_WIDTH = A_Q + 2 * A_KV + 2 * B_QK + B_V + 3 * C_W + 2 * D_WIDTH

kernel_name = 'hybrid_parallel_mixer_encoder'


def rms_norm(x, g, eps=RMS_EPS):
    xf = x.astype(jnp.float32)
    y = xf * lax.rsqrt(jnp.mean(xf * xf, axis=-1, keepdims=True) + eps)
    return (y * g.astype(jnp.float32)).astype(x.dtype)


def layer_norm(x, g, b, eps=LN_EPS):
    xf = x.astype(jnp.float32)
    mu = jnp.mean(xf, axis=-1, keepdims=True)
    xc = xf - mu
    y = xc * lax.rsqrt(jnp.mean(xc * xc, axis=-1, keepdims=True) + eps)
    return (y * g.astype(jnp.float32) + b.astype(jnp.float32)).astype(x.dtype)


def split_cols(y, sizes):
    outs, start = [], 0
    for n in sizes:
        outs.append(y[..., start:start + n])
        start += n
    return outs


def to_blocks(x):
    b, s = x.shape[:2]
    return jnp.swapaxes(x.reshape(b, s // Q_BLOCK, Q_BLOCK, *x.shape[2:]), 0, 1)


def from_blocks(y):
    y = jnp.swapaxes(y, 0, 1)
    return y.reshape(y.shape[0], -1, *y.shape[3:])


def axial_rope_tables(seq_len):
    t = jnp.arange(seq_len)
    row = (t // GRID_W).astype(jnp.float32)
    col = (t % GRID_W).astype(jnp.float32)
    n_freq = HEAD_DIM // 4
    inv = ROPE_THETA ** (-jnp.arange(n_freq, dtype=jnp.float32) / n_freq)
    ang = jnp.concatenate([row[:, None] * inv, col[:, None] * inv], axis=-1)
    return jnp.cos(ang), jnp.sin(ang)


def apply_rope(x, cos, sin):
    xf = x.astype(jnp.float32)
    half = HEAD_DIM // 2
    x1, x2 = xf[..., :half], xf[..., half:]
    c, s = cos[None, :, None, :], sin[None, :, None, :]
    return jnp.concatenate([x1 * c - x2 * s, x2 * c + x1 * s], axis=-1).astype(x.dtype)


def alibi_slopes(n):
    start = 2.0 ** (-8.0 / n)
    return start ** jnp.arange(1, n + 1, dtype=jnp.float32)


def gqa_axial_attention(q, k, v):
    b, s, _, dh = q.shape
    cos, sin = axial_rope_tables(s)
    q = apply_rope(q, cos, sin)
    k = apply_rope(k, cos, sin)
    qb = to_blocks(q.reshape(b, s, A_KV_HEADS, A_GROUP, dh) * (dh ** -0.5))

    def step(q_blk):
        sc = jnp.einsum('bqkgd,bskd->bkgqs', q_blk, k).astype(jnp.float32)
        pr = jax.nn.softmax(sc, axis=-1).astype(v.dtype)
        return jnp.einsum('bkgqs,bskd->bqkgd', pr, v)

    o = from_blocks(lax.map(step, qb))
    return o.reshape(b, s, A_HEADS * dh)


def diff_attention(q, k, v, lam, lam_init, g_sub):
    b, s = q.shape[:2]
    pos = jnp.arange(s, dtype=jnp.float32)
    slopes = alibi_slopes(B_HEADS)
    qb = to_blocks(q * (B_QK_DIM ** -0.5))
    pb = pos.reshape(s // Q_BLOCK, Q_BLOCK)

    def step(args):
        q_blk, q_pos = args
        sc = jnp.einsum('bqhmd,bshmd->bhmqs', q_blk, k).astype(jnp.float32)
        dist = jnp.abs(q_pos[:, None] - pos[None, :])
        sc = sc - slopes[None, :, None, None, None] * dist[None, None, None]
        pr = jax.nn.softmax(sc, axis=-1)
        a = pr[:, :, 0] - lam * pr[:, :, 1]
        return jnp.einsum('bhqs,bshd->bqhd', a.astype(v.dtype), v)

    o = from_blocks(lax.map(step, (qb, pb)))
    o = rms_norm(o, g_sub) * (1.0 - lam_init)
    return o.reshape(b, s, B_HEADS * HEAD_DIM)


def neighbourhood_attention(q, k, v, rpb):
    b, s, h, dh = q.shape
    rows = s // GRID_W
    wr = min(NA_ROWS, rows)
    qg = q.reshape(b, rows, GRID_W, h, dh).transpose(1, 0, 3, 2, 4)
    kg = k.reshape(b, rows, GRID_W, h, dh).transpose(0, 3, 1, 2, 4)
    vg = v.reshape(b, rows, GRID_W, h, dh).transpose(0, 3, 1, 2, 4)
    cols = jnp.arange(GRID_W)
    cs = jnp.clip(cols - NA_COLS // 2, 0, GRID_W - NA_COLS)
    col_idx = cs[:, None] + jnp.arange(NA_COLS)[None, :]
    dc = col_idx - cols[:, None] + (NA_COLS - 1)
    scale = dh ** -0.5

    def row_block(args):
        r, q_r = args
        rs = jnp.clip(r - wr // 2, 0, rows - wr)
        k_band = lax.dynamic_slice_in_dim(kg, rs, wr, axis=2)
        v_band = lax.dynamic_slice_in_dim(vg, rs, wr, axis=2)
        k_nb = k_band[:, :, :, col_idx, :]
        v_nb = v_band[:, :, :, col_idx, :]
        sc = jnp.einsum('bhqd,bhrqcd->bhqrc', q_r * scale, k_nb).astype(jnp.float32)
        dr = rs + jnp.arange(wr) - r + (NA_ROWS - 1)
        bias = rpb[:, dr[:, None, None], dc[None, :, :]]
        sc = sc + bias.transpose(0, 2, 1, 3).astype(jnp.float32)[None]
        pr = jax.nn.softmax(sc.reshape(b, h, GRID_W, wr * NA_COLS), axis=-1)
        pr = pr.reshape(b, h, GRID_W, wr, NA_COLS).astype(v.dtype)
        return jnp.einsum('bhqrc,bhrqcd->bhqd', pr, v_nb)

    o = lax.map(row_block, (jnp.arange(rows), qg))
    return o.transpose(1, 0, 3, 2, 4).reshape(b, s, h * dh)


def spatial_gating(uv, ln_g, ln_b, w_s, b_s):
    z = jax.nn.gelu(uv, approximate=False)
    u, vv = z[..., :D_WIDTH], z[..., D_WIDTH:]
    vv = layer_norm(vv, ln_g, ln_b)
    b, s, c = vv.shape
    vv = vv.reshape(b, s // D_CHUNK, D_CHUNK, D_GROUPS, c // D_GROUPS)
    sv = jnp.einsum('gts,bnsgc->bntgc', w_s, vv) + b_s.T[None, None, :, :, None]
    return u * sv.reshape(b, s, c)


def setup_inputs(seed: int = 0) -> dict:
    key = jax.random.key(seed)
    ks = jax.random.split(key, 24)
    f32 = jnp.float32
    nrm = lambda k, shape, sc: jax.random.normal(k, shape, f32) * sc
    gain = lambda k, shape: 1.0 + 0.02 * jax.random.normal(k, shape, f32)
    L = DEPTH
    return {
        'x': nrm(ks[0], (BATCH, SEQ, D_MODEL), 1.0),
        'p': nrm(ks[1], (DEPTH, BATCH, SEQ, PLE_DIM), 1.0),
        'g_mix': gain(ks[2], (L, D_MODEL)),
        'w_in': nrm(ks[3], (L, D_MODEL, PROJ_WIDTH), D_MODEL ** -0.5),
        'a_q_norm': gain(ks[4], (L, HEAD_DIM)),
        'a_k_norm': gain(ks[5], (L, HEAD_DIM)),
        'b_lam_q': nrm(ks[6], (L, 2, B_QK_DIM), 0.1),
        'b_lam_k': nrm(ks[7], (L, 2, B_QK_DIM), 0.1),
        'b_sub_norm': gain(ks[8], (L, HEAD_DIM)),
        'c_rpb': nrm(ks[9], (L, C_HEADS, 2 * NA_ROWS - 1, 2 * NA_COLS - 1), 0.1),
        'd_ln_g': gain(ks[10], (L, D_WIDTH)),
        'd_ln_b': nrm(ks[11], (L, D_WIDTH), 0.02),
        'd_w_s': nrm(ks[12], (L, D_GROUPS, D_CHUNK, D_CHUNK), D_CHUNK ** -0.5),
        'd_b_s': 1.0 + nrm(ks[13], (L, D_GROUPS, D_CHUNK), 0.1),
        'w_out': nrm(ks[14], (L, MIX_WIDTH, D_MODEL), MIX_WIDTH ** -0.5),
        'g_ffn': gain(ks[15], (L, D_MODEL)),
        'w_gate': nrm(ks[16], (L, D_MODEL, FFN_DIM), D_MODEL ** -0.5),
        'w_up': nrm(ks[17], (L, D_MODEL, FFN_DIM), D_MODEL ** -0.5),
        'w_down': nrm(ks[18], (L, FFN_DIM, D_MODEL), FFN_DIM ** -0.5),
        'g_ple': gain(ks[19], (L, D_MODEL)),
        'w_ple_gate': nrm(ks[20], (L, D_MODEL, D_MODEL), D_MODEL ** -0.5),
        'w_ple_proj': nrm(ks[21], (L, PLE_DIM, D_MODEL), PLE_DIM ** -0.5),
        'g_final': gain(ks[22], (D_MODEL,)),
    }


def reference(x, p, g_mix, w_in, a_q_norm, a_k_norm, b_lam_q, b_lam_k, b_sub_norm,
              c_rpb, d_ln_g, d_ln_b, d_w_s, d_b_s, w_out, g_ffn, w_gate, w_up, w_down,
              g_ple, w_ple_gate, w_ple_proj, g_final):
    b, s, _ = x.shape
    h = x
    for i in range(DEPTH):
        hn = rms_norm(h, g_mix[i])
        proj = hn @ w_in[i]
        aq, ak, av, bq, bk, bv, cq, ck, cv, duv = split_cols(proj, PROJ_SPLITS)

        aq = rms_norm(aq.reshape(b, s, A_HEADS, HEAD_DIM), a_q_norm[i])
        ak = rms_norm(ak.reshape(b, s, A_KV_HEADS, HEAD_DIM), a_k_norm[i])
        av = av.reshape(b, s, A_KV_HEADS, HEAD_DIM)
        ya = gqa_axial_attention(aq, ak, av)

        lam_init = 0.8 - 0.6 * math.exp(-0.3 * i)
        lq = b_lam_q[i].astype(jnp.float32)
        lk = b_lam_k[i].astype(jnp.float32)
        lam = jnp.exp(jnp.sum(lq[0] * lk[0])) - jnp.exp(jnp.sum(lq[1] * lk[1])) + lam_init
        yb = diff_attention(bq.reshape(b, s, B_HEADS, 2, B_QK_DIM),
                            bk.reshape(b, s, B_HEADS, 2, B_QK_DIM),
                            bv.reshape(b, s, B_HEADS, HEAD_DIM),
                            lam, lam_init, b_sub_norm[i])

        yc = neighbourhood_attention(cq.reshape(b, s, C_HEADS, HEAD_DIM),
                                     ck.reshape(b, s, C_HEADS, HEAD_DIM),
                                     cv.reshape(b, s, C_HEADS, HEAD_DIM), c_rpb[i])

        yd = spatial_gating(duv, d_ln_g[i], d_ln_b[i], d_w_s[i], d_b_s[i])

        mix = jnp.concatenate([ya, yb, yc, yd], axis=-1)
        h = h + mix @ w_out[i]

        hn = rms_norm(h, g_ffn[i])
        h = h + (jax.nn.silu(hn @ w_gate[i]) * (hn @ w_up[i])) @ w_down[i]

        gate = jax.nn.sigmoid(rms_norm(h, g_ple[i]) @ w_ple_gate[i])
        h = h + gate * (p[i] @ w_ple_proj[i])
    return rms_norm(h, g_final)
```

```python
from concourse.bass_utils import run_bass_kernel_spmd
import contextlib
import numpy as np
import concourse.bass as bass
import concourse.mybir as mybir

F32 = mybir.dt.float32
BF16 = mybir.dt.bfloat16
AF = mybir.ActivationFunctionType
ALU = mybir.AluOpType
AX = mybir.AxisListType

ENGS = ("pe", "act", "dve", "pool", "sp")


class Tok:
    __slots__ = ("sem", "val", "eng")

    def __init__(self, sem, val, eng):
        self.sem = sem
        self.val = val
        self.eng = eng


class DSem:
    def __init__(self, prog, name):
        self.sem = prog.new_sem(name)
        self.cnt = 0


class Prog:
    def __init__(self, nc):
        self.nc = nc
        self.stack = contextlib.ExitStack()
        self.q = {e: [] for e in ENGS}
        self.cnt = {e: 0 for e in ENGS}
        self.nsem = 0
        self.esem = {e: self.new_sem("c_" + e) for e in ENGS}
        self.lastw = {}
        self.readers = {}
        self.ninstr = 0

    def new_sem(self, name):
        self.nsem += 1
        return self.stack.enter_context(self.nc.semaphore(f"{name}_{self.nsem}"))

    def sbuf(self, name, shape, dt):
        return self.stack.enter_context(self.nc.sbuf_tensor(name, list(shape), dt))

    def psum(self, name, shape, dt=F32):
        return self.stack.enter_context(self.nc.psum_tensor(name, list(shape), dt))

    def _deps(self, eng, r, w, extra):
        toks = [t for t in extra if t is not None]
        for k in r:
            t = self.lastw.get(k)
            if t is not None and not (eng == "pe" and t.eng == "pe"):
                toks.append(t)
        for k in w:
            t = self.lastw.get(k)
            if t is not None and not (eng == "pe" and t.eng == "pe"):
                toks.append(t)
            for t in self.readers.get(k, ()):
                if not (eng == "pe" and t.eng == "pe"):
                    toks.append(t)
        best = {}
        for t in toks:
            k = id(t.sem)
            if k not in best or best[k][1] < t.val:
                best[k] = (t.sem, t.val)
        return list(best.values())

    def _commit(self, tok, r, w):
        for k in r:
            self.readers.setdefault(k, []).append(tok)
        for k in w:
            self.lastw[k] = tok
            self.readers[k] = []

    def op(self, eng, fn, r=(), w=(), deps=(), sig=True):
        waits = self._deps(eng, r, w, deps)
        inc = None
        tok = None
        if sig:
            self.cnt[eng] += 1
            inc = (self.esem[eng], 1)
            tok = Tok(self.esem[eng], self.cnt[eng], eng)
            self._commit(tok, r, w)
        self.q[eng].append((fn, waits, inc))
        return tok

    def mm(self, out, ops, r=(), w=(), deps=(), start=None, stop=None, **kw):
        waits = self._deps("pe", r, w, deps)
        n = len(ops)
        tok = None
        for i, (l, rr) in enumerate(ops):
            last = i == n - 1
            st = (i == 0) if start is None else (start and i == 0)
            sp_ = last if stop is None else (stop and last)
            inc = None
            if last:
                self.cnt["pe"] += 1
                inc = (self.esem["pe"], 1)
                tok = Tok(self.esem["pe"], self.cnt["pe"], "pe")
                self._commit(tok, r, w)
            self.q["pe"].append((
                (lambda e, l=l, rr=rr, s=st, p=sp_: e.matmul(out, l, rr, start=s, stop=p, **kw)),
                waits if i == 0 else [], inc))
        return tok

    def dma(self, eng, dsem, out, in_, r=(), w=(), deps=(), **kw):
        deps = list(deps)
        if dsem.cnt > 0:
            deps.append(Tok(dsem.sem, dsem.cnt, "dma"))
        waits = self._deps(eng, r, w, deps)
        dsem.cnt += 16
        tok = Tok(dsem.sem, dsem.cnt, "dma")
        self._commit(tok, r, w)
        self.q[eng].append((lambda e: e.dma_start(out=out, in_=in_, **kw), waits, (dsem.sem, 16)))
        return tok

    def wait(self, eng, toks):
        waits = self._deps(eng, (), (), toks)
        self.q[eng].append((None, waits, None))

    def emit(self):
        nc = self.nc
        with nc.Block() as block:
            def run(engname):
                def body(e):
                    waited = {}
                    for fn, waits, inc in self.q[engname]:
                        for sem, val in waits:
                            k = id(sem)
                            if waited.get(k, 0) >= val:
                                continue
                            waited[k] = val
                            e.wait_ge(sem, val)
                            self.ninstr += 1
                        if fn is None:
                            continue
                        ins = fn(e)
                        self.ninstr += 1
                        if inc is not None:
                            ins.then_inc(inc[0], inc[1])
                return body

            block.tensor(run("pe"))
            block.scalar(run("act"))
            block.vector(run("dve"))
            block.gpsimd(run("pool"))
            block.sync(run("sp"))

    def close(self):
        self.stack.close()

import math
import ml_dtypes

BF = ml_dtypes.bfloat16
OFF = dict(aq=0, ak=256, av=384, bq=512, bk=768, bv=1024, cq=1280, ck=1536, cv=1792, du=2048, dv=2304)


def _sw(cols):
    return cols.reshape(-1, 2, 32)[:, ::-1, :].reshape(-1)


def prep_win(w_in):
    ar = np.arange
    aq, ak = ar(0, 256), ar(256, 384)
    bq, bk = ar(512, 768), ar(768, 1024)
    cq, ck = ar(1280, 1536), ar(1536, 1792)
    du = ar(2048, 2304)
    order_f = np.concatenate([aq[:128], _sw(aq[:128]), aq[128:], _sw(aq[128:]), ak, _sw(ak),
                              bq[:128], bq[128:], bk[:128], bk[128:], cq[:128], cq[128:], ck[:128], ck[128:],
                              du[:128], du[128:]])
    order_t = np.concatenate([ar(384, 512), ar(1024, 1280), ar(1792, 2048), ar(2304, 2560)])
    return np.ascontiguousarray(w_in[:, order_f]), np.ascontiguousarray(w_in[:, order_t])


def wout_perm():
    perm = []
    for j in range(4):
        perm += list(range(j * 64, j * 64 + 64)) + list(range(256 + j * 64, 256 + j * 64 + 64))
    perm += list(range(512, 768)) + list(range(768, 1024))
    return np.array(perm)


def colvec(g):
    return np.ascontiguousarray(g.reshape(8, 128).T)


def prep_gA(gq, gk):
    p = np.arange(128) % 64
    ps = (p + 32) % 64
    return np.ascontiguousarray(np.stack([gq[p], gq[ps], gk[p], gk[ps]], axis=1).astype(np.float32))


def rope_tables(tok0, ntok):
    t = np.arange(tok0, tok0 + ntok)
    row = (t // 64).astype(np.float32)
    col = (t % 64).astype(np.float32)
    inv = (np.float32(10000.0) ** (-np.arange(16, dtype=np.float32) / np.float32(16))).astype(np.float32)
    ang = np.concatenate([row[:, None] * inv, col[:, None] * inv], axis=-1).astype(np.float32)
    cos = np.cos(ang).astype(np.float32).T
    sin = np.sin(ang).astype(np.float32).T
    cosf = np.concatenate([cos, cos, cos, cos], axis=0)
    sins = np.concatenate([-sin, sin, -sin, sin], axis=0)
    return np.ascontiguousarray(np.stack([cosf * np.float32(0.125), sins * np.float32(0.125), cosf, sins]).astype(np.float32))


def prep_D(ln_g, ln_b, w_s, b_s):
    lnbc = np.ascontiguousarray(np.stack([np.broadcast_to(ln_g, (128, 256)), np.broadcast_to(ln_b, (128, 256))]).astype(np.float32))
    wsT = np.ascontiguousarray(w_s.transpose(2, 0, 1).astype(np.float32))
    p = np.arange(128)
    f = np.arange(512)
    bsbc = np.stack([b_s[2 * c + p[:, None] // 64, f[None, :] % 128] for c in range(2)], axis=1)
    return lnbc, wsT, np.ascontiguousarray(bsbc.astype(np.float32))


def mix_consts(j, S):
    slope = np.float32(2.0 ** (-2.0 * (j + 1)))
    f = np.arange(512)
    hi = (slope * 16 * (f // 16)).astype(np.float32)
    lo = (slope * (f % 16)).astype(np.float32)
    baug = np.stack([np.stack([-hi, -lo]), np.stack([hi, lo])]).astype(BF)
    p = np.arange(128)[:, None]
    n = np.arange(128)[None, :]
    tabs = np.stack([-slope * 128.0 * n + 0.0 * p, -slope * 128.0 * n + 0.0 * p], axis=1).astype(np.float32)
    dstr = np.stack([-slope * np.abs(128.0 * m + p - f[None, :]) for m in range(4)] + [slope * (p - f[None, :]), -slope * (p - f[None, :])], axis=1).astype(np.float32)
    onesk = np.ones((2, S), dtype=BF)
    return dict(baug=baug, tabs=np.ascontiguousarray(tabs), dstr=np.ascontiguousarray(dstr), onesk=onesk)


def c_tiles(rpb_h, S):
    R = S // 64
    npair = R // 2
    a = np.arange(128) // 64
    kc = np.arange(128) % 64
    jj = np.arange(512) // 128
    b = (np.arange(512) % 128) // 64
    qc = np.arange(512) % 64
    vals = np.zeros((15, 128, 512), np.float32)
    mask = np.zeros((15, 128, 512), np.float32)
    for cset, i0 in enumerate((0, 4, npair - 4)):
        for s in range(5):
            i = i0 + jj
            m = np.clip(i - 2, 0, npair - 5) + s
            kr = (2 * m)[None, :] + a[:, None]
            r = (2 * i + b)[None, :]
            rs = np.clip(r - 4, 0, R - 8)
            cs = np.clip(qc - 8, 0, 48)[None, :]
            valid = (kr >= rs) & (kr < rs + 8) & (kc[:, None] >= cs) & (kc[:, None] < cs + 16)
            dr = np.clip(kr - r + 7, 0, 14)
            dc = np.clip(kc[:, None] - qc[None, :] + 15, 0, 30)
            g = rpb_h[dr, dc]
            vals[cset * 5 + s] = np.where(valid, g, np.float32(0))
            mask[cset * 5 + s] = np.where(valid, np.float32(0), np.float32(-30000.0))
    return vals, mask


D = 1024
FF = 2816
TT = 512
RMS_EPS = 1e-6
LN_EPS = 1e-5
WSLOT = 22528

NF = 16
NTV = 896


def build_dense(ntok, do_post, do_proj, do_final):
    nc = bass.Bass("TRN2", target_bir_lowering=False)
    P = Prog(nc)
    ntile = ntok // TT

    def din(name, shape, dt=F32):
        return nc.dram_tensor(name, list(shape), dt, kind="ExternalInput").ap()

    def dout(name, shape, dt=F32):
        return nc.dram_tensor(name, list(shape), dt, kind="ExternalOutput").ap()

    hT_in = din("hT_in", [D, ntok])
    if do_post:
        mixT = din("mixT", [D, ntok], BF16)
        pT = din("pT", [256, ntok])
        wout = din("wout", [D, D])
        wg = din("wg", [D, FF])
        wu = din("wu", [D, FF])
        wd = din("wd", [FF, D])
        wpg = din("wpg", [D, D])
        wpp = din("wpp", [256, D])
    gcols = din("gcols", [128, 4, 8])
    if do_proj:
        winf = din("winf", [D, NF * 128])
        wint = din("wint", [D, NTV])
        gA = din("gA", [128, 4])
        rope = din("rope", [4, 128, ntok])
        lnbc = din("lnbc", [2, 128, 256])
        wsT = din("wsT", [128, 4, 128])
        bsbc = din("bsbc", [128, 2, 512])
        qk_out = dout("qk_out", [4, 6, 64, ntok], BF16)
        v_out = dout("v_out", [ntok, 640], BF16)
        yd_out = dout("yd_out", [256, ntok], BF16)
    if do_post:
        h_out = dout("h_out", [D, ntok])

    wslot = [P.sbuf(f"wslot{i}", [128, WSLOT], BF16) for i in range(2)]
    wsem = [DSem(P, f"wsem{i}") for i in range(2)]
    h = P.sbuf("h", [128, 8, TT], F32)
    hn = P.sbuf("hn", [128, 8, TT], BF16)
    sq = P.sbuf("sq", [128, 8, TT], BF16)
    std = P.sbuf("std", [128, TT], F32)
    rstd = P.sbuf("rstd", [128, TT], F32)
    gc = P.sbuf("gc", [128, 4, 8], F32)
    ones_bf = P.sbuf("ones_bf", [128, 128], BF16)
    csem = DSem(P, "csem")
    hsem = DSem(P, "hsem")
    osem = [DSem(P, f"osem{i}") for i in range(4)]
    P.op("pool", lambda e: e.memset(ones_bf[:], 1.0), w=["ones"])
    P.dma("sp", csem, gc[:], gcols, w=["gc"])
    ckeys = ["gc"]
    banks = [P.psum(f"bank{i}", [128, 512]) for i in range(8)]
    bstate = {"i": 0}

    reserved = set()

    def bank(reserve=False):
        while bstate["i"] in reserved:
            bstate["i"] = (bstate["i"] + 1) % 8
        i = bstate["i"]
        bstate["i"] = (i + 1) % 8
        if reserve:
            reserved.add(i)
        return banks[i], ("bank", i)

    if do_post:
        mix = P.sbuf("mix", [128, 8, TT], BF16)
        pt = P.sbuf("pt", [128, 2, TT], BF16)
        act = P.sbuf("act", [128, 22, TT], BF16)
        silu = P.sbuf("silu", [128, 2, TT], F32)
        msem = DSem(P, "msem")
        psem = DSem(P, "psem")
    if do_proj:
        blk = P.sbuf("blk", [128, 128], BF16)
        gAs = P.sbuf("gAs", [128, 4], F32)
        ropes = P.sbuf("ropes", [128, 4, TT], F32)
        lnb = P.sbuf("lnb", [128, 2, 256], F32)
        wss = P.sbuf("wss", [128, 4, 128], BF16)
        bss = P.sbuf("bss", [128, 2, TT], F32)
        uT = P.sbuf("uT", [128, 2, TT], F32)
        t1 = P.sbuf("t1", [128, 2, TT], F32)
        t2 = P.sbuf("t2", [128, 2, TT], F32)
        stg = [P.sbuf(f"stg{i}", [128, TT], BF16) for i in range(4)]
        vstg = [P.sbuf(f"vstg{i}", [128, 640], BF16) for i in range(2)]
        vsem = [DSem(P, f"vsem{i}") for i in range(2)]
        z = P.sbuf("z", [128, 256], F32)
        zn = P.sbuf("zn", [128, 256], BF16)
        bst = P.sbuf("bst", [128, 6], F32)
        mv = P.sbuf("mv", [128, 2], F32)
        lstd = P.sbuf("lstd", [128, 1], F32)
        ydT = P.sbuf("ydT", [128, 2, TT], BF16)
        ydtmp = P.sbuf("ydtmp", [128, 2, TT], F32)
        rsem = DSem(P, "rsem")
        ysem = DSem(P, "ysem")
        P.op("pool", lambda e: e.memset(blk[:], 0.0), w=["blk"])
        P.op("pool", lambda e: e.memset(blk[0:64, 0:64], 1.0), w=["blk"], r=["blk"])
        P.op("pool", lambda e: e.memset(blk[64:128, 64:128], 1.0), w=["blk"], r=["blk"])
        P.dma("sp", csem, gAs[:], gA, w=["gAs"])
        P.dma("sp", csem, lnb[:], lnbc.rearrange("a p n -> p a n"), w=["lnb"])
        csem2 = DSem(P, "csem2")
        P.dma("pool", csem2, wss[:], wsT, w=["wss"])
        P.dma("sp", csem, bss[:], bsbc, w=["bss"])
        ckeys += ["gAs", "lnb", "bss"]

    for k in ckeys:
        P.lastw[k] = Tok(csem.sem, csem.cnt, "dma")
    hosem = DSem(P, "hosem")
    pssv_r = [bank(reserve=True), bank(reserve=True)] if do_proj else None
    wstate = {"n": 0}

    def load_unit(parts):
        s = wstate["n"] % 2
        wstate["n"] += 1
        key = ("wslot", s)
        views = []
        off = 0
        for i, (ap, kch, ncols) in enumerate(parts):
            v = wslot[s][:, off:off + kch * ncols].rearrange("p (k n) -> p k n", k=kch)
            P.dma("pool", wsem[s], v, ap.rearrange("(k p) n -> p k n", p=128), w=[key])
            views.append(v)
            off += kch * ncols
        assert off <= WSLOT
        return views, key

    def norm(gi, tag):
        for kc in range(8):
            P.op("act", lambda e, kc=kc: e.activation(sq[:, kc, :], h[:, kc, :], AF.Square),
                 r=[("h", kc)], w=[("sq", kc)])
        ps, bk = bank()
        P.mm(ps[:], [(ones_bf[:], sq[:, kc, :]) for kc in range(8)],
             r=["ones"] + [("sq", kc) for kc in range(8)], w=[bk])
        P.op("act", lambda e: e.activation(std[:], ps[:], AF.Sqrt, bias=RMS_EPS, scale=1.0 / D),
             r=[bk], w=["std"])
        P.op("dve", lambda e: e.reciprocal(rstd[:], std[:]), r=["std"], w=["rstd"])

    def apply_norm(gi, dst_is_hn=True):
        for kc in range(8):
            P.op("dve", lambda e, kc=kc: e.scalar_tensor_tensor(
                hn[:, kc, :], h[:, kc, :], gc[:, gi, kc:kc + 1], rstd[:], ALU.mult, ALU.mult),
                r=[("h", kc), "gc", "rstd"], w=[("hn", kc)])

    for ti in range(ntile):
        tsl = slice(ti * TT, (ti + 1) * TT)
        P.dma("sp", hsem, h[:], hT_in[:, tsl].rearrange("(k p) t -> p k t", p=128),
              w=[("h", kc) for kc in range(8)])
        if do_post:
            P.dma("sp", msem, mix[:], mixT[:, tsl].rearrange("(k p) t -> p k t", p=128),
                  w=[("mix", kc) for kc in range(8)])
            P.dma("pool", psem, pt[:], pT[:, tsl].rearrange("(k p) t -> p k t", p=128), w=["pt"])
            (wo,), wk = load_unit([(wout, 8, D)])
            for n in range(8):
                ps, bk = bank()
                P.mm(ps[:], [(wo[:, kc, n * 128:(n + 1) * 128], mix[:, kc, :]) for kc in range(8)],
                     r=[wk] + [("mix", kc) for kc in range(8)], w=[bk])
                P.op("dve", lambda e, n=n, ps=ps: e.tensor_tensor(h[:, n, :], h[:, n, :], ps[:], ALU.add),
                     r=[bk, ("h", n)], w=[("h", n)])
            norm(0, "ffn")
            apply_norm(0)
            HALF = FF // 2
            for half in range(2):
                cs = slice(half * HALF, (half + 1) * HALF)
                (wgs, wus), wk = load_unit([(wg[:, cs], 8, HALF), (wu[:, cs], 8, HALF)])
                for j in range(HALF // 128):
                    n = half * (HALF // 128) + j
                    psg, bkg = bank()
                    P.mm(psg[:], [(wgs[:, kc, j * 128:(j + 1) * 128], hn[:, kc, :]) for kc in range(8)],
                         r=[wk] + [("hn", kc) for kc in range(8)], w=[bkg])
                    psu, bku = bank()
                    P.mm(psu[:], [(wus[:, kc, j * 128:(j + 1) * 128], hn[:, kc, :]) for kc in range(8)],
                         r=[wk] + [("hn", kc) for kc in range(8)], w=[bku])
                    sb = n % 2
                    P.op("act", lambda e, sb=sb, psg=psg: e.activation(silu[:, sb, :], psg[:], AF.Silu),
                         r=[bkg], w=[("silu", sb)])
                    P.op("dve", lambda e, sb=sb, psu=psu, n=n: e.tensor_tensor(
                        act[:, n, :], silu[:, sb, :], psu[:], ALU.mult),
                        r=[bku, ("silu", sb)], w=[("act", n)])
            (wds,), wk = load_unit([(wd, 22, D)])
            for n in range(8):
                ps, bk = bank()
                P.mm(ps[:], [(wds[:, kc, n * 128:(n + 1) * 128], act[:, kc, :]) for kc in range(22)],
                     r=[wk] + [("act", kc) for kc in range(22)], w=[bk])
                P.op("dve", lambda e, n=n, ps=ps: e.tensor_tensor(h[:, n, :], h[:, n, :], ps[:], ALU.add),
                     r=[bk, ("h", n)], w=[("h", n)])
            norm(1, "ple")
            apply_norm(1)
            (wpgs, wpps), wk = load_unit([(wpg, 8, D), (wpp, 2, D)])
            for n in range(8):
                psg, bkg = bank()
                P.mm(psg[:], [(wpgs[:, kc, n * 128:(n + 1) * 128], hn[:, kc, :]) for kc in range(8)],
                     r=[wk] + [("hn", kc) for kc in range(8)], w=[bkg])
                psp, bkp = bank()
                P.mm(psp[:], [(wpps[:, kc, n * 128:(n + 1) * 128], pt[:, kc, :]) for kc in range(2)],
                     r=[wk, "pt"], w=[bkp])
                sb = n % 2
                P.op("act", lambda e, sb=sb, psg=psg: e.activation(silu[:, sb, :], psg[:], AF.Sigmoid),
                     r=[bkg], w=[("silu", sb)])
                P.op("dve", lambda e, sb=sb, psp=psp: e.tensor_tensor(
                    silu[:, sb, :], silu[:, sb, :], psp[:], ALU.mult),
                    r=[bkp, ("silu", sb)], w=[("silu", sb)])
                P.op("pool", lambda e, sb=sb, n=n: e.tensor_tensor(
                    h[:, n, :], h[:, n, :], silu[:, sb, :], ALU.add),
                    r=[("silu", sb), ("h", n)], w=[("h", n)])
            if do_final:
                norm(3, "fin")
                for kc in range(8):
                    P.op("dve", lambda e, kc=kc: e.scalar_tensor_tensor(
                        h[:, kc, :], h[:, kc, :], gc[:, 3, kc:kc + 1], rstd[:], ALU.mult, ALU.mult),
                        r=[("h", kc), "gc", "rstd"], w=[("h", kc)])
            P.dma("sp", hosem, h_out[:, tsl].rearrange("(k p) t -> p k t", p=128), h[:],
                  r=[("h", kc) for kc in range(8)])
        if do_proj:
            norm(2, "mix")
            apply_norm(2)
            P.dma("sp", rsem, ropes[:], rope[:, :, tsl].rearrange("a p t -> p a t"), w=["ropes"])
            (wf,), wk = load_unit([(winf, 8, NF * 128)])
            hnr = [("hn", kc) for kc in range(8)]

            def proj_f(c):
                ps, bk = bank()
                P.mm(ps[:], [(wf[:, kc, c * 128:(c + 1) * 128], hn[:, kc, :]) for kc in range(8)],
                     r=[wk] + hnr, w=[bk])
                return ps, bk

            stgi = {"i": 0}

            def stage_out(fn, rkeys, dests):
                i = stgi["i"] % 4
                stgi["i"] += 1
                eng, f = fn
                P.op(eng, lambda e, i=i: f(e, stg[i][:]), r=rkeys, w=[("stg", i)])
                for (p0, p1, dap) in dests:
                    P.dma("sp", osem[i], dap, stg[i][p0:p1, :], r=[("stg", i)])

            for (c, cs_, gcol, gsw, ci, si, typ) in ((0, 1, 0, 1, 0, 1, 0), (2, 3, 0, 1, 0, 1, 0), (4, 5, 2, 3, 2, 3, 1)):
                psq, bkq = proj_f(c)
                pss, bks = proj_f(cs_)
                sb = (c // 2) % 2
                P.op("act", lambda e, psq=psq, sb=sb: e.activation(sq[:, sb, :], psq[:], AF.Square),
                     r=[bkq], w=[("sq", sb)])
                psn, bkn = bank()
                P.mm(psn[:], [(blk[:], sq[:, sb, :])], r=["blk", ("sq", sb)], w=[bkn])
                P.op("act", lambda e, psn=psn: e.activation(std[:], psn[:], AF.Sqrt, bias=RMS_EPS, scale=1.0 / 64),
                     r=[bkn], w=["std"])
                P.op("dve", lambda e: e.reciprocal(rstd[:], std[:]), r=["std"], w=["rstd"])
                P.op("dve", lambda e, psq=psq, sb=sb, gcol=gcol, ci=ci: e.scalar_tensor_tensor(
                    t1[:, sb, :], psq[:], gAs[:, gcol:gcol + 1], ropes[:, ci, :], ALU.mult, ALU.mult),
                    r=[bkq, "gAs", "ropes"], w=[("t1", sb)])
                P.op("dve", lambda e, pss=pss, sb=sb, gsw=gsw, si=si: e.scalar_tensor_tensor(
                    t2[:, sb, :], pss[:], gAs[:, gsw:gsw + 1], ropes[:, si, :], ALU.mult, ALU.mult),
                    r=[bks, "gAs", "ropes"], w=[("t2", sb)])
                P.op("pool", lambda e, sb=sb: e.tensor_tensor(t1[:, sb, :], t1[:, sb, :], t2[:, sb, :], ALU.add),
                     r=[("t1", sb), ("t2", sb)], w=[("t1", sb)])
                if typ == 0:
                    hh = c
                    dests = [(0, 64, qk_out[hh, 0, :, tsl]), (64, 128, qk_out[hh + 1, 0, :, tsl])]
                else:
                    dests = [(0, 64, qk_out[0, 1, :, tsl]), (0, 64, qk_out[1, 1, :, tsl]),
                             (64, 128, qk_out[2, 1, :, tsl]), (64, 128, qk_out[3, 1, :, tsl])]
                stage_out(("dve", lambda e, o, sb=sb: e.tensor_tensor(o, t1[:, sb, :], rstd[:], ALU.mult)),
                          [("t1", sb), "rstd"], dests)
            for (c, typ, hh, scale) in ((6, 2, 0, 32 ** -0.5), (7, 2, 2, 32 ** -0.5), (8, 3, 0, 1.0), (9, 3, 2, 1.0),
                                         (10, 4, 0, 0.125), (11, 4, 2, 0.125), (12, 5, 0, 1.0), (13, 5, 2, 1.0)):
                ps, bk = proj_f(c)
                stage_out(("act", lambda e, o, ps=ps, scale=scale: e.activation(o, ps[:], AF.Copy, scale=scale)),
                          [bk], [(0, 64, qk_out[hh, typ, :, tsl]), (64, 128, qk_out[hh + 1, typ, :, tsl])])
            for j in range(2):
                ps, bk = proj_f(14 + j)
                P.op("act", lambda e, ps=ps, j=j: e.activation(uT[:, j, :], ps[:], AF.Gelu),
                     r=[bk], w=[("uT", j)])
            (wt,), wkt = load_unit([(wint, 8, NTV)])
            pssv = pssv_r
            for s in range(4):
                ssl = slice(s * 128, (s + 1) * 128)
                ps0, bk0 = bank()
                P.mm(ps0[:, 0:512], [(hn[:, kc, ssl], wt[:, kc, 0:512]) for kc in range(8)],
                     r=[wkt] + hnr, w=[bk0])
                ps1, bk1 = bank()
                P.mm(ps1[:, 0:384], [(hn[:, kc, ssl], wt[:, kc, 512:896]) for kc in range(8)],
                     r=[wkt] + hnr, w=[bk1])
                vi = s % 2
                P.op("act", lambda e, vi=vi, ps0=ps0: e.activation(vstg[vi][:, 0:512], ps0[:, 0:512], AF.Copy),
                     r=[bk0], w=[("vstg", vi, 0)])
                P.op("act", lambda e, vi=vi, ps1=ps1: e.activation(vstg[vi][:, 512:640], ps1[:, 0:128], AF.Copy),
                     r=[bk1], w=[("vstg", vi, 1)])
                P.dma("sp", vsem[vi], v_out[ti * TT + s * 128: ti * TT + (s + 1) * 128, :], vstg[vi][:],
                      r=[("vstg", vi, 0), ("vstg", vi, 1)])
                P.op("act", lambda e, ps1=ps1: e.activation(z[:], ps1[:, 128:384], AF.Gelu), r=[bk1], w=["z"])
                P.op("dve", lambda e: e.bn_stats(bst[:], z[:]), r=["z"], w=["bst"])
                P.op("dve", lambda e: e.bn_aggr(mv[:], bst[:]), r=["bst"], w=["mv"])
                P.op("act", lambda e: e.activation(lstd[:], mv[:, 1:2], AF.Sqrt, bias=LN_EPS, scale=1.0),
                     r=["mv"], w=["lstd"])
                P.op("dve", lambda e: e.reciprocal(lstd[:], lstd[:]), r=["lstd"], w=["lstd"])
                P.op("dve", lambda e: e.tensor_scalar(z[:], z[:], mv[:, 0:1], lstd[:, 0:1], ALU.subtract, ALU.mult),
                     r=["z", "mv", "lstd"], w=["z"])
                P.op("dve", lambda e: e.tensor_tensor(z[:], z[:], lnb[:, 0, :], ALU.mult), r=["z", "lnb"], w=["z"])
                P.op("dve", lambda e: e.tensor_tensor(zn[:], z[:], lnb[:, 1, :], ALU.add), r=["z", "lnb"], w=["zn"])
                for g in range(4):
                    pv, bkv = pssv[g // 2]
                    po = (g % 2) * 64
                    P.mm(pv[po:po + 64, ssl], [(zn[:, g * 64:(g + 1) * 64], wss[:, g, :])],
                         r=["zn", "wss"], w=[(bkv, g % 2, s)])
            for j in range(2):
                pv, bkv = pssv[j]
                P.op("dve", lambda e, j=j, pv=pv: e.tensor_tensor(ydtmp[:, j, :], pv[:], bss[:, j, :], ALU.add),
                     r=[(bkv, a, s) for a in range(2) for s in range(4)] + ["bss"], w=[("ydtmp", j), bkv])
                P.op("pool", lambda e, j=j: e.tensor_tensor(ydT[:, j, :], ydtmp[:, j, :], uT[:, j, :], ALU.mult),
                     r=[("ydtmp", j), ("uT", j)], w=[("ydT", j)])
            P.dma("sp", ysem, yd_out[:, tsl].rearrange("(k p) t -> p k t", p=128), ydT[:],
                  r=[("ydT", 0), ("ydT", 1)])

    allsem = osem + [hosem] + ([ysem] + vsem if do_proj else [])
    for eng in ("sp",):
        P.wait(eng, [Tok(d.sem, d.cnt, "dma") for d in allsem if d.cnt > 0])
    P.emit()
    P.close()
    return nc, P


QB = 512
LOOK = 2
NSB = 4
NPB = 4


def build_mix(S, parts='abc'):
    nc = bass.Bass("TRN2", target_bir_lowering=False)
    P = Prog(nc)
    nqb = S // QB
    nkb = S // 128
    npair = S // 128
    ngrp = npair // 4

    def din(name, shape, dt=F32):
        return nc.dram_tensor(name, list(shape), dt, kind="ExternalInput").ap()

    qk = din("qk", [6, 64, S], BF16)
    v = din("v", [S, 3, 64], BF16)
    baug = din("baug", [2, 2, QB], BF16)
    onesk = din("onesk", [2, S], BF16)
    tabs_d = din("tabs", [128, 2, 128])
    dstr_d = din("dstr", [128, 6, QB])
    crpb_d = din("crpb", [15, 128, QB])
    cmask_d = din("cmask", [15, 128, QB])
    lqk_d = din("lqk", [64, 128])
    lami_d = din("lami", [64, 1])
    gsub_d = din("gsub", [64, 1])
    oml_d = din("oml", [64, 1])
    yT = nc.dram_tensor("yT", [192, S], BF16, kind="ExternalOutput").ap()

    kac = P.sbuf("kac", [128, S], BF16)
    kb_ = P.sbuf("kb_", [128, S], BF16)
    V = P.sbuf("V", [128, nkb, 3, 65], BF16)
    qa = [P.sbuf(f"qa{i}", [128, QB], BF16) for i in range(2)]
    qb_ = [[P.sbuf(f"qb{v_}{i}", [128, QB], BF16) for i in range(2)] for v_ in range(2)]
    pbuf = [P.sbuf(f"pb{i}", [128, QB], BF16) for i in range(NPB)]
    sfb = [P.sbuf(f"sfb{i}", [128, QB], F32) for i in range(2)]
    tabs = P.sbuf("tabs_s", [128, 2, 128], F32)
    dstr = P.sbuf("dstr_s", [128, 6, QB], F32)
    cbias = P.sbuf("cbias", [128, 15, QB], F32)
    ctmp = [P.sbuf(f"ctmp{i}", [128, QB], F32) for i in range(2)]
    ones_f = P.sbuf("ones_f", [128, 64], F32)
    ones_b = P.sbuf("ones_b", [64, 64], BF16)
    osb = [P.sbuf(f"osb{i}", [65, QB], F32) for i in range(2)]
    hi = [P.sbuf(f"hi{i}", [65, QB], BF16) for i in range(2)]
    lo = [P.sbuf(f"lo{i}", [65, QB], BF16) for i in range(2)]
    rden = [P.sbuf(f"rden{i}", [64, QB], F32) for i in range(2)]
    sel = P.sbuf("sel", [65, 64], BF16)
    a1 = P.sbuf("a1", [64, QB], F32)
    a2 = P.sbuf("a2", [64, QB], F32)
    sqb = P.sbuf("sqb", [64, QB], BF16)
    stdb = P.sbuf("stdb", [64, QB], F32)
    ystg = [P.sbuf(f"ystg{i}", [64, QB], BF16) for i in range(2)]
    lqk = P.sbuf("lqk_s", [64, 128], F32)
    lsm = P.sbuf("lsm", [64, 8], F32)
    lamcol = P.sbuf("lamcol", [64, 1], F32)
    gso = P.sbuf("gso", [64, 2], F32)

    sbank = [P.psum(f"sbk{i}", [128, 512]) for i in range(NSB)]
    obank = [P.psum(f"obk{i}", [128, 512]) for i in range(4)]

    csem = DSem(P, "csem")
    ksem = DSem(P, "ksem")
    vsem = DSem(P, "vsem")
    qsem = [DSem(P, f"qsem{i}") for i in range(2)]
    qsemb = [DSem(P, f"qsemb{i}") for i in range(2)]
    qsemc = [DSem(P, f"qsemc{i}") for i in range(2)]
    ysem = [DSem(P, f"ysem{i}") for i in range(2)]
    cmsem = [DSem(P, f"cmsem{i}") for i in range(2)]

    P.op("pool", lambda e: e.memset(ones_f[:], 1.0), w=["ones_f"])
    P.op("pool", lambda e: e.memset(ones_b[:], 1.0), w=["ones_b"])
    P.op("pool", lambda e: e.memset(sel[:], 0.0), w=["sel"])
    P.op("pool", lambda e: e.memset(sel[64:65, :], 1.0), r=["sel"], w=["sel"])
    P.op("pool", lambda e: e.memset(V[:, :, :, 64:65], 1.0), w=["Vones"])
    zt = P.op("pool", lambda e: e.memset(kb_[:], 0.0), w=["kbz"])
    for ver in range(2):
        for i in range(2):
            zt = P.op("pool", lambda e, ver=ver, i=i: e.memset(qb_[ver][i][:], 0.0), w=[("qbz", ver, i)])
    P.wait("sp", [zt])
    P.dma("sp", ksem, kac[0:64, :], qk[1], w=[("K", 0)])
    P.dma("sp", ksem, kac[64:128, :], qk[5], w=[("K", 1)])
    P.dma("sp", ksem, kb_[0:32, :], qk[3, 0:32, :], w=[("K", 2)])
    P.dma("sp", ksem, kb_[64:96, :], qk[3, 32:64, :], w=[("K", 3)])
    P.dma("sp", ksem, kb_[32:34, :], onesk, w=[("K", 4)])
    P.dma("sp", ksem, kb_[96:98, :], onesk, w=[("K", 5)])
    vr = v.rearrange("(n p) m d -> p n m d", p=128)
    step = max(1, nkb // 8)
    for m in range(3):
        for n0 in range(0, nkb, step):
            P.dma("sp", vsem, V[:, n0:n0 + step, m, 0:64], vr[:, n0:n0 + step, m, :], w=[("V", m, n0)])
    P.dma("sp", csem, tabs[:], tabs_d, w=["tabs"])
    P.dma("sp", csem, dstr[:], dstr_d, w=["dstr"])
    P.dma("sp", csem, lqk[:], lqk_d, w=["lqk"])
    P.dma("sp", csem, lsm[:, 4:5], lami_d, w=["lsm"])
    P.dma("sp", csem, gso[:, 0:1], gsub_d, w=["gso0"])
    P.dma("sp", csem, gso[:, 1:2], oml_d, w=["gso1"])
    for ver in range(2):
        for i in range(2):
            P.dma("sp", csem, qb_[ver][i][32:34, :], baug[ver], w=[("qbaug0", ver, i)])
            P.dma("sp", csem, qb_[ver][i][96:98, :], baug[ver], w=[("qbaug1", ver, i)])
    ctok = Tok(csem.sem, csem.cnt, "dma")
    for k in ["tabs", "dstr", "lqk", "lsm", "gso"] + [("qbaug", a, b) for a in range(2) for b in range(2)]:
        P.lastw[k] = ctok
    P.lastw["K"] = Tok(ksem.sem, ksem.cnt, "dma")
    P.lastw["V"] = Tok(vsem.sem, vsem.cnt, "dma")
    cbsem = DSem(P, "cbsem")
    if 'c' in parts:
        for t in range(15):
            P.dma("sp", cbsem, cbias[:, t, :], crpb_d[t], w=[("cbias", t)])
        cbtok = Tok(cbsem.sem, cbsem.cnt, "dma")
        for t in range(15):
            P.lastw[("cbias", t)] = cbtok
        for t in range(15):
            i = t % 2
            P.dma("sp", cmsem[i], ctmp[i][:], cmask_d[t], w=[("ctmp", i)])
            P.op("pool", lambda e, t=t, i=i: e.tensor_tensor(cbias[:, t, :], cbias[:, t, :], ctmp[i][:], ALU.add),
                 r=[("ctmp", i), ("cbias", t)], w=[("cbias", t)])

    P.op("dve", lambda e: e.tensor_tensor(lqk[:, 0:64], lqk[:, 0:64], lqk[:, 64:128], ALU.mult), r=["lqk"], w=["lqk"])
    P.op("dve", lambda e: e.tensor_reduce(lsm[:, 0:2], lqk[:, 0:64].rearrange("p (a b) -> p a b", a=2), AX.X, ALU.add),
         r=["lqk"], w=["lsm01"])
    P.op("act", lambda e: e.activation(lsm[:, 2:4], lsm[:, 0:2], AF.Exp), r=["lsm01"], w=["lsm23"])
    P.op("dve", lambda e: e.tensor_tensor(lsm[:, 5:6], lsm[:, 2:3], lsm[:, 3:4], ALU.subtract), r=["lsm23"], w=["lsm5"])
    P.op("dve", lambda e: e.tensor_tensor(lsm[:, 6:7], lsm[:, 5:6], lsm[:, 4:5], ALU.add), r=["lsm5", "lsm"], w=["lsm6"])
    P.op("dve", lambda e: e.tensor_copy(lamcol[:], lsm[:, 6:7]), r=["lsm6"], w=["lamcol"])
    P.op("dve", lambda e: e.tensor_tensor(gso[:, 0:1], gso[:, 0:1], gso[:, 1:2], ALU.mult), r=["gso"], w=["gso"])

    st = {"n": 0, "pend": [], "y": 0}

    def tile(s_ops, s_r, post, pv_list):
        n = st["n"]
        st["n"] += 1
        si = n % NSB
        pi = n % NPB
        ps = sbank[si]
        sbkey = ("sb", si)
        for (osl, l, r_) in s_ops:
            P.mm(osl(ps), [(l, r_)], r=s_r, w=[sbkey])
        post(ps, sbkey, pbuf[pi], ("P", pi))
        st["pend"].append((pi, pv_list))
        if len(st["pend"]) > LOOK:
            flush_one()

    def flush_one():
        pi, pv_list = st["pend"].pop(0)
        for (oap, lf, okey, s0, s1) in pv_list:
            l, r_ = lf(pbuf[pi])
            P.mm(oap, [(l, r_)], r=[("P", pi), "V", "Vones"], w=[okey], start=s0, stop=s1)

    def flush_all():
        while st["pend"]:
            flush_one()

    def den_bcast(ob, okey, slot, srcs=None):
        if srcs is None:
            srcs = [(ob[0:65, :], okey, 0, QB)]
        for (sap, sk, c0, wd_) in srcs:
            P.op("act", lambda e, sap=sap, c0=c0, wd_=wd_: e.activation(osb[slot][:, c0:c0 + wd_], sap, AF.Copy),
                 r=[sk], w=[("osb", slot)])
        P.op("dve", lambda e: e.tensor_copy(hi[slot][:], osb[slot][:]), r=[("osb", slot)], w=[("hi", slot)])
        P.op("dve", lambda e: e.tensor_tensor(lo[slot][:], osb[slot][:], hi[slot][:], ALU.subtract),
             r=[("osb", slot), ("hi", slot)], w=[("lo", slot)])
        n = st["n"]
        st["n"] += 1
        si = n % NSB
        ps = sbank[si]
        P.mm(ps[0:64, :], [(sel[:], hi[slot][:]), (sel[:], lo[slot][:])],
             r=["sel", ("hi", slot), ("lo", slot)], w=[("sb", si)])
        P.op("dve", lambda e: e.reciprocal(rden[slot][:], ps[0:64, :]), r=[("sb", si)], w=[("rden", slot)])

    def store_y(fn, rkeys, row0, col0, width=QB):
        i = st["y"] % 2
        st["y"] += 1
        eng, f = fn
        P.op(eng, lambda e, i=i: f(e, ystg[i][:, 0:width]), r=rkeys, w=[("ystg", i)])
        P.dma("pool", ysem[i], yT[row0:row0 + 64, col0:col0 + width], ystg[i][:, 0:width], r=[("ystg", i)])

    def simple_epilogue(ob, okey, row0, col0, srcs=None):
        den_bcast(ob, okey, 0, srcs)
        store_y(("dve", lambda e, o: e.tensor_tensor(o, osb[0][0:64, :], rden[0][:], ALU.mult)),
                [("osb", 0), ("rden", 0)], row0, col0)

    for q in range(nqb if 'a' in parts else 0):
        qi = q % 2
        P.dma("sp", qsem[qi], qa[qi][0:64, :], qk[0, :, q * QB:(q + 1) * QB], w=[("qa", qi)])
        ob = obank[q % 2]
        okey = ("ob", q % 2)
        for k in range(nkb):
            tile([(lambda ps: ps[:, :], kac[0:64, k * 128:(k + 1) * 128], qa[qi][0:64, :])],
                 ["K", ("qa", qi)],
                 lambda ps, sk, pb, pk: P.op("act", lambda e: e.activation(pb[:], ps[:], AF.Exp), r=[sk], w=[pk]),
                 [(ob[0:65, :], (lambda pb, k=k: (V[:, k, 0, :], pb[:])), okey, k == 0, k == nkb - 1)])
        flush_all()
        simple_epilogue(ob, okey, 0, q * QB)

    for q in range(nqb if 'b' in parts else 0):
        qi = q % 2
        P.dma("sp", qsemb[qi], qb_[0][qi][0:32, :], qk[2, 0:32, q * QB:(q + 1) * QB], w=[("qb", qi)])
        P.dma("sp", qsemb[qi], qb_[0][qi][64:96, :], qk[2, 32:64, q * QB:(q + 1) * QB], w=[("qb", qi)])
        obs = [obank[2 * (q % 2) + m] for m in range(2)]
        okeys = [("ob", 2 * (q % 2) + m) for m in range(2)]
        for k in range(nkb):
            n = k - 4 * q
            for m in range(2):
                r0 = 64 * m
                if n < 0 or n >= 4:
                    ver = 0 if n < 0 else 1
                    col = -n if n < 0 else n
                    lhs = kb_[r0:r0 + 32, k * 128:(k + 1) * 128]
                    rhs = qb_[0][qi][r0:r0 + 32, :]

                    def post(ps, sk, pb, pk, ver=ver, col=col):
                        fi = st["n"] % 2
                        P.op("dve", lambda e: e.scalar_tensor_tensor(
                            sfb[fi][:], ps[:], tabs[:, ver, col:col + 1], dstr[:, 4 + ver, :], ALU.add, ALU.add),
                            r=[sk, "dstr", "tabs"], w=[("sfb", fi)])
                        P.op("act", lambda e: e.activation(pb[:], sfb[fi][:], AF.Exp), r=[("sfb", fi)], w=[pk])
                    rk = ["K", ("qb", qi)]
                else:
                    lhs = kb_[r0:r0 + 32, k * 128:(k + 1) * 128]
                    rhs = qb_[0][qi][r0:r0 + 32, :]

                    def post(ps, sk, pb, pk, n=n):
                        fi = st["n"] % 2
                        P.op("dve", lambda e: e.tensor_tensor(sfb[fi][:], ps[:], dstr[:, n, :], ALU.add),
                             r=[sk, "dstr"], w=[("sfb", fi)])
                        P.op("act", lambda e: e.activation(pb[:], sfb[fi][:], AF.Exp), r=[("sfb", fi)], w=[pk])
                    rk = ["K", ("qb", qi)]
                tile([(lambda ps: ps[:, :], lhs, rhs)], rk, post,
                     [(obs[m][0:65, :], (lambda pb, k=k: (V[:, k, 1, :], pb[:])), okeys[m], k == 0, k == nkb - 1)])
        flush_all()
        den_bcast(obs[0], okeys[0], 0)
        den_bcast(obs[1], okeys[1], 1)
        P.op("dve", lambda e: e.tensor_tensor(a1[:], osb[0][0:64, :], rden[0][:], ALU.mult),
             r=[("osb", 0), ("rden", 0)], w=["a1"])
        P.op("dve", lambda e: e.scalar_tensor_tensor(a2[:], osb[1][0:64, :], lamcol[:, 0:1], rden[1][:], ALU.mult, ALU.mult),
             r=[("osb", 1), ("rden", 1), "lamcol"], w=["a2"])
        P.op("pool", lambda e: e.tensor_tensor(a1[:], a1[:], a2[:], ALU.subtract), r=["a1", "a2"], w=["a1"])
        P.op("act", lambda e: e.activation(sqb[:], a1[:], AF.Square), r=["a1"], w=["sqb"])
        nn = st["n"]
        st["n"] += 1
        psn = sbank[nn % NSB]
        pkn = ("sb", nn % NSB)
        P.mm(psn[0:64, :], [(ones_b[:], sqb[:])], r=["ones_b", "sqb"], w=[pkn])
        P.op("act", lambda e, psn=psn: e.activation(stdb[:], psn[0:64, :], AF.Sqrt, bias=RMS_EPS, scale=1.0 / 64), r=[pkn], w=["stdb"])
        P.op("dve", lambda e: e.reciprocal(stdb[:], stdb[:]), r=["stdb"], w=["stdb"])
        store_y(("dve", lambda e, o: e.scalar_tensor_tensor(o, a1[:], gso[:, 0:1], stdb[:], ALU.mult, ALU.mult)),
                ["a1", "gso", "stdb"], 64, q * QB)

    for g in range(ngrp if 'c' in parts else 0):
        qi = g % 2
        P.dma("sp", qsemc[qi], qa[qi][64:128, :], qk[4, :, g * QB:(g + 1) * QB], w=[("qa", qi)])
        cset = 0 if g == 0 else (2 if g == ngrp - 1 else 1)
        ob = obank[g % 2]
        okey = ("ob", g % 2)
        for s in range(5):
            ms = [min(max(4 * g + jj - 2, 0), npair - 5) + s for jj in range(4)]
            s_ops = [((lambda ps, jj=jj: ps[:, jj * 128:(jj + 1) * 128]),
                      kac[64:128, ms[jj] * 128:(ms[jj] + 1) * 128],
                      qa[qi][64:128, jj * 128:(jj + 1) * 128]) for jj in range(4)]
            t = cset * 5 + s

            def post(ps, sk, pb, pk, t=t):
                fi = st["n"] % 2
                P.op("dve", lambda e: e.tensor_tensor(sfb[fi][:], ps[:], cbias[:, t, :], ALU.add),
                     r=[sk, ("cbias", t)], w=[("sfb", fi)])
                P.op("act", lambda e: e.activation(pb[:], sfb[fi][:], AF.Exp), r=[("sfb", fi)], w=[pk])
            pv = [(obank[jj][0:65, 0:128],
                   (lambda pb, jj=jj, mm_=ms[jj]: (V[:, mm_, 2, :], pb[:, jj * 128:(jj + 1) * 128])),
                   ("ob", jj), s == 0, s == 4) for jj in range(4)]
            tile(s_ops, ["K", ("qa", qi)], post, pv)
        flush_all()
        simple_epilogue(None, None, 128, g * QB,
                        srcs=[(obank[jj][0:65, 0:128], ("ob", jj), jj * 128, 128) for jj in range(4)])

    P.wait("pool", [Tok(d.sem, d.cnt, "dma") for d in ysem if d.cnt > 0])
    P.emit()
    P.close()
    return nc, P


_CACHE = {}


def _get(name, fn):
    if name not in _CACHE:
        _CACHE[name] = fn()[0]
    return _CACHE[name]


def kernel(**inp):
    f32 = np.float32
    A = {k: np.asarray(v, dtype=f32) for k, v in inp.items()}
    x, p = A["x"], A["p"]
    NB, S, DM = x.shape
    NT = S // 4
    L = 4
    cores = [(b, t) for b in range(NB) for t in range(4)]
    ids = list(range(8))
    perm = wout_perm()
    ropes = [rope_tables(t * NT, NT) for t in range(4)]
    zeros_g = np.zeros((128, 8), f32)

    def proj_inputs(i):
        winf, wint = prep_win(A["w_in"][i])
        lnbc, wsT, bsbc = prep_D(A["d_ln_g"][i], A["d_ln_b"][i], A["d_w_s"][i], A["d_b_s"][i])
        return dict(winf=winf, wint=wint, gA=prep_gA(A["a_q_norm"][i], A["a_k_norm"][i]), lnbc=lnbc, wsT=wsT, bsbc=bsbc)

    def post_inputs(i):
        return dict(wout=np.ascontiguousarray(A["w_out"][i][perm]), wg=A["w_gate"][i], wu=A["w_up"][i],
                    wd=A["w_down"][i], wpg=A["w_ple_gate"][i], wpp=A["w_ple_proj"][i])

    hT = [np.ascontiguousarray(x[b, t * NT:(t + 1) * NT, :].T) for (b, t) in cores]
    nc0 = _get("first", lambda: build_dense(NT, False, True, False))
    pi = proj_inputs(0)
    gcl = np.ascontiguousarray(np.stack([zeros_g, zeros_g, colvec(A["g_mix"][0]), zeros_g], axis=1))
    maps = [dict(hT_in=hT[c], gcols=gcl, rope=ropes[cores[c][1]], **pi) for c in ids]
    res = run_bass_kernel_spmd(nc0, maps, core_ids=ids).results
    out = None
    for i in range(L):
        ncm = _get("mix", lambda: build_mix(S))
        lam_init = 0.8 - 0.6 * math.exp(-0.3 * i)
        maps = []
        for (b, j) in cores:
            src = [res[b * 4 + t] for t in range(4)]
            qk = np.ascontiguousarray(np.concatenate([np.asarray(r["qk_out"])[j] for r in src], axis=-1))
            vs = [np.asarray(r["v_out"]) for r in src]
            v = np.concatenate([np.stack([vv[:, (j // 2) * 64:(j // 2) * 64 + 64], vv[:, 128 + j * 64:128 + j * 64 + 64],
                                          vv[:, 384 + j * 64:384 + j * 64 + 64]], axis=1) for vv in vs], axis=0)
            cv, cm = c_tiles(A["c_rpb"][i, j], S)
            m = dict(qk=qk, v=np.ascontiguousarray(v), crpb=cv, cmask=cm,
                     lqk=np.ascontiguousarray(np.broadcast_to(np.concatenate([A["b_lam_q"][i].reshape(-1), A["b_lam_k"][i].reshape(-1)])[None, :], (64, 128)).astype(f32)),
                     lami=np.full((64, 1), lam_init, f32), gsub=np.ascontiguousarray(A["b_sub_norm"][i][:, None]),
                     oml=np.full((64, 1), 1.0 - lam_init, f32))
            m.update(mix_consts(j, S))
            maps.append(m)
        resm = run_bass_kernel_spmd(ncm, maps, core_ids=ids).results
        last = i == L - 1
        ncd = _get("last" if last else "mid", lambda: build_dense(NT, True, not last, last))
        po = post_inputs(i)
        gcl = np.ascontiguousarray(np.stack([colvec(A["g_ffn"][i]), colvec(A["g_ple"][i]),
                                             zeros_g if last else colvec(A["g_mix"][i + 1]), colvec(A["g_final"])], axis=1))
        pi = {} if last else proj_inputs(i + 1)
        maps = []
        for c, (b, t) in enumerate(cores):
            ys = [np.asarray(resm[b * 4 + j]["yT"])[:, t * NT:(t + 1) * NT] for j in range(4)]
            mixT = np.concatenate([ys[0][0:128], ys[1][0:128], ys[2][0:128], ys[3][0:128],
                                   ys[0][128:192], ys[1][128:192], ys[2][128:192], ys[3][128:192],
                                   np.asarray(res[c]["yd_out"])], axis=0)
            m = dict(hT_in=hT[c], mixT=np.ascontiguousarray(mixT), pT=np.ascontiguousarray(p[i, b, t * NT:(t + 1) * NT, :].T),
                     gcols=gcl, **po, **pi)
            if not last:
                m["rope"] = ropes[t]
            maps.append(m)
        res = run_bass_kernel_spmd(ncd, maps, core_ids=ids).results
        hT = [np.asarray(r["h_out"]) for r in res]
    out = np.empty((NB, S, DM), f32)
    for c, (b, t) in enumerate(cores):
        out[b, t * NT:(t + 1) * NT, :] = hT[c].T
    return out
```
